# Optimizing a Trainium2 kernel written in Bass

```python
import jax
import jax.numpy as jnp
from jax import lax
import numpy as np

D_MODEL = 2048
BATCH = 1
SEQ = 8192
DEPTH = 2

GLA_V = D_MODEL // 2
GLA_DV = 128
GLA_HEADS = GLA_V // GLA_DV
GLA_DK = GLA_DV // 2
GLA_QK = GLA_HEADS * GLA_DK
GLA_GATE_RANK = 16
GLA_TAU = 16.0
GLA_CHUNK = 64

RWKV_WIDTH = D_MODEL - GLA_V
RWKV_HEAD = 64
RWKV_HEADS = RWKV_WIDTH // RWKV_HEAD
RWKV_W_RANK = 96
RWKV_A_RANK = 96
RWKV_G_RANK = 256
RWKV_V_RANK = 64
RWKV_GN_EPS = 64e-5

D_MIX = GLA_V + RWKV_WIDTH
D_FF = 4 * D_MODEL
RMS_EPS = 1e-6

GLA_SPLITS = (GLA_QK, GLA_QK, GLA_V, GLA_V, GLA_GATE_RANK)
RWKV_SPLITS = (RWKV_WIDTH, RWKV_WIDTH, RWKV_WIDTH, RWKV_W_RANK, RWKV_A_RANK, RWKV_G_RANK)
GLA_COLS = sum(GLA_SPLITS)
RWKV_COLS = sum(RWKV_SPLITS)
N_IN = GLA_COLS + RWKV_COLS

kernel_name = 'hymba_gla_rwkv7_hybrid'


def rmsnorm(x, g):
    xf = x.astype(jnp.float32)
    y = xf * lax.rsqrt(jnp.mean(xf * xf, axis=-1, keepdims=True) + RMS_EPS)
    return (y * g.astype(jnp.float32)).astype(x.dtype)


def token_shift(z):
    return jnp.pad(z, ((0, 0), (1, 0), (0, 0)))[:, :-1]


def lerp_shift(z, mu):
    return z + mu * (token_shift(z) - z)


def split_cols(z, sizes):
    return jnp.split(z, np.cumsum(sizes)[:-1].tolist(), axis=-1)


def gla_mixer(q, k, v, g, a_low, w_a_up, b_a, norm_w):
    f32 = jnp.float32
    B, T, _ = q.shape
    H, DK, DV, C = GLA_HEADS, GLA_DK, GLA_DV, GLA_CHUNK
    n_chunks = T // C
    log_a = jax.nn.log_sigmoid(a_low.astype(f32) @ w_a_up.astype(f32) + b_a.astype(f32)) / GLA_TAU

    def to_chunks(z, d):
        return z.astype(f32).reshape(B, n_chunks, C, H, d).transpose(1, 0, 3, 2, 4)

    qc = to_chunks(q, DK) * (DK ** -0.5)
    kc = to_chunks(k, DK)
    vc = to_chunks(v, DV)
    lac = to_chunks(log_a, DK)
    causal = jnp.tril(jnp.ones((C, C), dtype=bool))[None, None, :, :, None]

    def chunk_step(S, inp):
        qi, ki, vi, lai = inp
        b = jnp.cumsum(lai, axis=2)
        o_inter = jnp.einsum('bhck,bhkv->bhcv', qi * jnp.exp(b), S)
        rel = b[:, :, :, None, :] - b[:, :, None, :, :]
        decay = jnp.where(causal, jnp.exp(jnp.minimum(rel, 0.0)), 0.0)
        scores = jnp.einsum('bhik,bhjk,bhijk->bhij', qi, ki, decay)
        o_intra = jnp.einsum('bhij,bhjv->bhiv', scores, vi)
        b_last = b[:, :, -1:, :]
        S = S * jnp.exp(b_last[:, :, 0, :, None]) + jnp.einsum('bhck,bhcv->bhkv', ki * jnp.exp(b_last - b), vi)
        return S, o_inter + o_intra

    S0 = jnp.zeros((B, H, DK, DV), f32)
    _, o = lax.scan(chunk_step, S0, (qc, kc, vc, lac))
    o = o.transpose(1, 0, 3, 2, 4).reshape(B, T, H, DV)
    o = o * lax.rsqrt(jnp.mean(o * o, axis=-1, keepdims=True) + RMS_EPS)
    o = o.reshape(B, T, GLA_V) * norm_w.astype(f32)
    return (o * jax.nn.silu(g.astype(f32))).astype(q.dtype)


def rwkv7_mixer(r, k, v, w_low, a_low, g_low, w0, w_up, a0, a_up, g_up, k_k, k_a, r_k, gn_w, gn_b):
    f32 = jnp.float32
    B, T, _ = r.shape
    H, N = RWKV_HEADS, RWKV_HEAD
    out_dtype = r.dtype
    r, k, v, w_low, a_low, g_low = (z.astype(f32) for z in (r, k, v, w_low, a_low, g_low))
    w = -jax.nn.softplus(-(w0.astype(f32) + jnp.tanh(w_low) @ w_up.astype(f32))) - 0.5
    decay = jnp.exp(-jnp.exp(w))
    a = jax.nn.sigmoid(a0.astype(f32) + a_low @ a_up.astype(f32))
    gate = jax.nn.sigmoid(g_low) @ g_up.astype(f32)
    kk = (k * k_k.astype(f32)).reshape(B, T, H, N)
    kk = kk * lax.rsqrt(jnp.maximum(jnp.sum(kk * kk, axis=-1, keepdims=True), 1e-24))
    k = k * (1.0 + (a - 1.0) * k_a.astype(f32))
    rh = r.reshape(B, T, H, N)
    kh = k.reshape(B, T, H, N)
    vh = v.reshape(B, T, H, N)
    wh = decay.reshape(B, T, H, N)
    ah = a.reshape(B, T, H, N)

    def tm(z):
        return jnp.moveaxis(z, 1, 0)

    def step(S, inp):
        r_t, w_t, k_t, v_t, a_t, b_t = inp
        sa = jnp.einsum('bhvk,bhk->bhv', S, a_t)
        S = S * w_t[:, :, None, :] + sa[..., None] * b_t[:, :, None, :] + v_t[..., None] * k_t[:, :, None, :]
        return S, jnp.einsum('bhvk,bhk->bhv', S, r_t)

    S0 = jnp.zeros((B, H, N, N), f32)
    _, y = lax.scan(step, S0, (tm(rh), tm(wh), tm(kh), tm(vh), tm(-kk), tm(kk * ah)))
    y = jnp.moveaxis(y, 0, 1)
    mu = jnp.mean(y, axis=-1, keepdims=True)
    var = jnp.mean(jnp.square(y - mu), axis=-1, keepdims=True)
    y = ((y - mu) * lax.rsqrt(var + RWKV_GN_EPS)).reshape(B, T, RWKV_WIDTH)
    y = y * gn_w.astype(f32) + gn_b.astype(f32)
    bonus = jnp.sum(rh * kh * r_k.astype(f32), axis=-1, keepdims=True) * vh
    y = y + bonus.reshape(B, T, RWKV_WIDTH)
    return (y * gate).astype(out_dtype)


def setup_inputs(seed: int = 0) -> dict:
    key = jax.random.key(seed)
    ks = list(jax.random.split(key, 40))
    it = iter(ks)

    def nrm(shape, scale):
        return scale * jax.random.normal(next(it), shape, jnp.float32)

    def unif(shape):
        return jax.random.uniform(next(it), shape, jnp.float32)

    D, L, RW = D_MODEL, DEPTH, RWKV_WIDTH
    ch = jnp.linspace(0.0, 1.0, RW, dtype=jnp.float32)
    return {
        'x': nrm((BATCH, SEQ, D), 1.0),
        'c': nrm((BATCH, D), 1.0),
        'w_ada': nrm((L, D, 6 * D), 0.5 * D ** -0.5),
        'b_ada': nrm((L, 6 * D), 0.02),
        'g_pre_mix': 1.0 + nrm((L, D), 0.02),
        'g_post_mix': 1.0 + nrm((L, D), 0.02),
        'g_pre_ffn': 1.0 + nrm((L, D), 0.02),
        'g_post_ffn': 1.0 + nrm((L, D), 0.02),
        'w_in': nrm((L, D, N_IN), D ** -0.5),
        'gla_w_a_up': nrm((L, GLA_GATE_RANK, GLA_QK), GLA_GATE_RANK ** -0.5),
        'gla_b_a': 1.0 + nrm((L, GLA_QK), 0.5),
        'gla_norm_w': 1.0 + nrm((L, GLA_V), 0.02),
        'rwkv_mu': unif((L, RWKV_COLS)),
        'rwkv_w0': -6.5 + 5.0 * ch ** 0.85 + nrm((L, RW), 0.1),
        'rwkv_w_up': nrm((L, RWKV_W_RANK, RW), 0.5 * RWKV_W_RANK ** -0.5),
        'rwkv_a0': nrm((L, RW), 0.1),
        'rwkv_a_up': nrm((L, RWKV_A_RANK, RW), RWKV_A_RANK ** -0.5),
        'rwkv_g_up': nrm((L, RWKV_G_RANK, RW), RWKV_G_RANK ** -0.5),
        'rwkv_k_k': 0.85 + nrm((L, RW), 0.02),
        'rwkv_k_a': 1.0 + nrm((L, RW), 0.02),
        'rwkv_r_k': nrm((L, RWKV_HEADS, RWKV_HEAD), 0.1),
        'rwkv_gn_w': 1.0 + nrm((L, RW), 0.02),
        'rwkv_gn_b': nrm((L, RW), 0.02),
        'vres_w_down': nrm((L - 1, D, RWKV_V_RANK), D ** -0.5),
        'vres_mu': unif((L - 1, RWKV_V_RANK)),
        'vres_up': nrm((L - 1, RWKV_V_RANK, RW), RWKV_V_RANK ** -0.5),
        'vres_v0': 1.0 + nrm((L - 1, RW), 0.1),
        'w_out': nrm((L, D_MIX, D), D_MIX ** -0.5),
        'w_ff1': nrm((L, D, D_FF), D ** -0.5),
        'w_ff2': nrm((L, D_FF, D), D_FF ** -0.5),
    }


def reference(x, c, w_ada, b_ada, g_pre_mix, g_post_mix, g_pre_ffn, g_post_ffn, w_in,
              gla_w_a_up, gla_b_a, gla_norm_w, rwkv_mu, rwkv_w0, rwkv_w_up, rwkv_a0, rwkv_a_up,
              rwkv_g_up, rwkv_k_k, rwkv_k_a, rwkv_r_k, rwkv_gn_w, rwkv_gn_b, vres_w_down, vres_mu,
              vres_up, vres_v0, w_out, w_ff1, w_ff2):
    cond = jax.nn.silu(c)
    v_first = None
    for i in range(DEPTH):
        mod = cond @ w_ada[i] + b_ada[i]
        sh1, sc1, gt1, sh2, sc2, gt2 = (m[:, None, :] for m in jnp.split(mod, 6, axis=-1))

        h = rmsnorm(x, g_pre_mix[i]) * (1.0 + sc1) + sh1
        z = h @ w_in[i]
        z_gla = z[..., :GLA_COLS]
        z_rwkv = lerp_shift(z[..., GLA_COLS:], rwkv_mu[i])
        gq, gk, gv, gg, ga = split_cols(z_gla, GLA_SPLITS)
        rr, rk, rv, rw, ra, rg = split_cols(z_rwkv, RWKV_SPLITS)
        if i == 0:
            v_first = rv
        else:
            j = i - 1
            vl = lerp_shift(h @ vres_w_down[j], vres_mu[j])
            rv = rv + (v_first - rv) * jax.nn.sigmoid(vres_v0[j] + vl @ vres_up[j])
        o_gla = gla_mixer(gq, gk, gv, gg, ga, gla_w_a_up[i], gla_b_a[i], gla_norm_w[i])
        o_rwkv = rwkv7_mixer(rr, rk, rv, rw, ra, rg, rwkv_w0[i], rwkv_w_up[i], rwkv_a0[i], rwkv_a_up[i],
                             rwkv_g_up[i], rwkv_k_k[i], rwkv_k_a[i], rwkv_r_k[i], rwkv_gn_w[i], rwkv_gn_b[i])
        y = jnp.concatenate([o_gla, o_rwkv], axis=-1) @ w_out[i]
        x = x + gt1 * rmsnorm(y, g_post_mix[i])

        h = rmsnorm(x, g_pre_ffn[i]) * (1.0 + sc2) + sh2
        y = jnp.square(jax.nn.relu(h @ w_ff1[i])) @ w_ff2[i]
        x = x + gt2 * rmsnorm(y, g_post_ffn[i])
    return x
```

```python
import contextlib
import numpy as np
import concourse.bass as bass
import concourse.mybir as mybir
from concourse.alu_op_type import AluOpType as ALU
from concourse.bass_utils import run_bass_kernel_spmd

F32 = mybir.dt.float32
BF16 = mybir.dt.bfloat16
AF = mybir.ActivationFunctionType
AX = mybir.AxisListType

ENGS = ("pe", "act", "dve", "pool", "sp")
N_DMA_SEMS = 32

D = 2048
KC = 16
DFF = 8192
NCORES = 8
SEQ = 8192
EPS = 1e-6
GN_EPS = 64e-5


class T:
    __slots__ = ("name", "t", "w", "r")

    def __init__(self, name, t):
        self.name = name
        self.t = t
        self.w = None
        self.r = {}

    def __getitem__(self, idx):
        return self.t[idx]


class Sched:
    def __init__(self, nc, stack):
        self.nc = nc
        self.stack = stack
        self.prog = {e: [] for e in ENGS}
        self.sems = {}
        for e in ENGS:
            self.sems[e] = stack.enter_context(nc.semaphore("c_" + e))
        self.cnt = {e: 0 for e in ENGS}
        self.dsem = []
        for i in range(N_DMA_SEMS):
            self.dsem.append(stack.enter_context(nc.semaphore("d%d" % i)))
            self.sems[("d", i)] = self.dsem[i]
        self.dcnt = [0] * N_DMA_SEMS
        self.dnext = 0
        self.waited = {e: {} for e in ENGS}
        self.rr = 0
        self.outs = []

    def sb(self, name, shape, dt=F32, stack=None):
        t = (stack or self.stack).enter_context(self.nc.sbuf_tensor("s_" + name, list(shape), dt))
        return T(name, t)

    def ps(self, name, shape, dt=F32):
        t = self.stack.enter_context(self.nc.psum_tensor("p_" + name, list(shape), dt))
        return T(name, t)

    def _wait(self, eng, semkey, val):
        if val <= 0:
            return
        if self.waited[eng].get(semkey, 0) >= val:
            return
        self.waited[eng][semkey] = val
        sem = self.sems[semkey]
        self.prog[eng].append(lambda e, sem=sem, val=val: e.wait_ge(sem, val))

    def _deps(self, eng, reads, writes):
        for t in reads:
            if t.w is not None:
                k, v, we = t.w
                if we == eng and (eng == "pe" or v > self.cnt[eng]):
                    continue
                self._wait(eng, k, v)
        for t in writes:
            if t.w is not None:
                k, v, we = t.w
                if we != eng:
                    self._wait(eng, k, v)
            for k, (v, re_) in t.r.items():
                if re_ == eng:
                    continue
                self._wait(eng, k, v)

    def op(self, eng, fn, reads=(), writes=(), signal=True):
        self._deps(eng, reads, writes)
        if signal:
            self.cnt[eng] += 1
            sem = self.sems[eng]
            self.prog[eng].append(lambda e, fn=fn, sem=sem: fn(e).then_inc(sem, 1))
            val = self.cnt[eng]
        else:
            self.prog[eng].append(lambda e, fn=fn: fn(e))
            val = self.cnt[eng] + 1
        for t in writes:
            t.w = (eng, val, eng)
            t.r = {}
        for t in reads:
            if t in writes:
                continue
            pv = t.r.get(eng, (0, eng))[0]
            t.r[eng] = (max(pv, val), eng)

    def ev(self, fn_act, fn_dve, reads=(), writes=()):
        self.rr += 1
        if self.rr % 2 == 0 and fn_act is not None:
            self.op("act", fn_act, reads, writes)
        else:
            self.op("dve", fn_dve, reads, writes)

    def dma(self, fn, reads=(), writes=(), q="sp", out=False):
        i = self.dnext
        self.dnext = (self.dnext + 1) % N_DMA_SEMS
        key = ("d", i)
        self._wait(q, key, self.dcnt[i])
        for t in reads:
            if t.w is not None:
                self._wait(q, t.w[0], t.w[1])
        for t in writes:
            if t.w is not None:
                self._wait(q, t.w[0], t.w[1])
            for k, (v, re_) in t.r.items():
                self._wait(q, k, v)
        self.dcnt[i] += 16
        val = self.dcnt[i]
        sem = self.dsem[i]
        self.prog[q].append(lambda e, fn=fn, sem=sem: fn(e).then_inc(sem, 16))
        for t in writes:
            t.w = (key, val, "dma")
            t.r = {}
        for t in reads:
            if t in writes:
                continue
            t.r[key] = (val, "dma")
        if out:
            self.outs.append((key, val))

    def barrier(self):
        for e in ("pe", "act", "dve", "pool", "sp"):
            for o in ("pe", "act", "dve", "pool"):
                if o != e:
                    self._wait(e, o, self.cnt[o])
            for i in range(N_DMA_SEMS):
                self._wait(e, ("d", i), self.dcnt[i])

    def finish(self):
        for key, val in self.outs:
            self._wait("sp", key, val)
        for i in range(N_DMA_SEMS):
            self._wait("sp", ("d", i), self.dcnt[i])

    def emit(self):
        nc = self.nc
        prog = self.prog
        with nc.Block() as block:
            @block.tensor
            def _(e):
                for f in prog["pe"]:
                    f(e)

            @block.scalar
            def _(e):
                for f in prog["act"]:
                    f(e)

            @block.vector
            def _(e):
                for f in prog["dve"]:
                    f(e)

            @block.gpsimd
            def _(e):
                for f in prog["pool"]:
                    f(e)

            @block.sync
            def _(e):
                for f in prog["sp"]:
                    f(e)


def bcast_ap(ap, dims):
    a = ap.ap
    return bass.AP(ap.tensor, ap.offset, [list(a[0])] + [list(d) for d in dims])


class PsumPool:
    def __init__(self, S, n=8):
        self.S = S
        self.banks = [S.ps("pb%d" % i, [128, 512]) for i in range(n)]
        self.i = 0

    def get(self):
        b = self.banks[self.i]
        self.i = (self.i + 1) % len(self.banks)
        return b


class VecPack:
    def __init__(self):
        self.cols = []
        self.off = {}

    def add(self, name, v):
        v = np.asarray(v, np.float32).reshape(-1)
        n = (len(v) + 127) // 128
        pad = np.zeros(n * 128, np.float32)
        pad[: len(v)] = v
        self.off[name] = (len(self.cols), n)
        for c in range(n):
            self.cols.append(pad[c * 128:(c + 1) * 128])

    def array(self):
        return np.ascontiguousarray(np.stack(self.cols, axis=1))


P2_VECS = ["nb_a", "gnorm_w", "mu_r", "mu_k", "mu_v", "mu_w", "mu_a", "mu_g0", "mu_g1", "mu_vr", "w0", "a0",
           "k_k", "k_a", "omka", "r_k", "gn_w", "gn_b", "v0"]
P2V = {n: i for i, n in enumerate(P2_VECS)}
LD_SCALE = -0.6065306597126334


def p2_consts(TB):
    i128 = np.eye(128, dtype=np.float32)
    i64s = np.concatenate([np.eye(64), np.eye(64)], 0).astype(np.float32)
    bo = np.zeros((128, 128), np.float32)
    bo[:64, :64] = 1
    bo[64:, 64:] = 1
    s = np.arange(64)[:, None]
    t = np.arange(64)[None, :]
    up_strict = (s < t).astype(np.float32)
    up_incl = (s <= t).astype(np.float32)
    mk1 = np.concatenate([np.concatenate([up_strict, up_incl], 1)] * 2, 0)
    mk3 = np.concatenate([(t < s).astype(np.float32)] * 2, 0)
    j = np.arange(128)[:, None]
    i = np.arange(128)[None, :]
    mkg = (j <= i).astype(np.float32)
    r64 = np.ones((128, TB), np.float32)
    r64[:, ::64] = 0
    r128 = np.ones((128, TB), np.float32)
    r128[:, ::128] = 0
    return np.ascontiguousarray(np.concatenate([i128, i64s, bo, mk1, mk3, mkg, r64, r128], 1))


def build_p2(TS, TB, layer):
    NB = TS // TB
    NCk = TB // 64
    NG = TB // 128
    HB = min(512, TB)
    NH = TB // HB
    nc = bass.Bass("TRN2", target_bir_lowering=False)
    zin = nc.dram_tensor("zin", [13 * 128, TS], F32, kind="ExternalInput")
    vecs_d = nc.dram_tensor("vecs", [128, len(P2_VECS)], F32, kind="ExternalInput")
    cst_d = nc.dram_tensor("cst", [128, 640 + 2 * TB], F32, kind="ExternalInput")
    lora_d = nc.dram_tensor("lora", [128, 7, 128], F32, kind="ExternalInput")
    vfi = nc.dram_tensor("vfin", [128, TS], F32, kind="ExternalInput") if layer > 0 else None
    og_d = nc.dram_tensor("og", [128, TS], F32, kind="ExternalOutput")
    orw_d = nc.dram_tensor("orw", [128, TS], F32, kind="ExternalOutput")
    vfo = nc.dram_tensor("vfout", [128, TS], F32, kind="ExternalOutput") if layer == 0 else None
    import os
    DBG = os.environ.get("P2DBG")
    dbg_d = nc.dram_tensor("dbg", [128, TB + 1], F32, kind="ExternalOutput") if DBG else None

    with contextlib.ExitStack() as st:
        S = Sched(nc, st)
        PP = PsumPool(S, 8)
        vecs = S.sb("vecs", [128, len(P2_VECS)])
        cst = S.sb("cst", [128, 640 + 2 * TB])
        lora = S.sb("lora", [128, 7, 128])
        S.dma(lambda e: e.dma_start(out=vecs[:, :], in_=vecs_d.ap()), writes=[vecs])
        S.dma(lambda e: e.dma_start(out=cst[:, :], in_=cst_d.ap()), writes=[cst])
        S.dma(lambda e: e.dma_start(out=lora[:, :, :], in_=lora_d.ap()), writes=[lora])

        i_nb, i_om = P2V["nb_a"], P2V["omka"]
        S.op("act", lambda e: e.mul(out=vecs[:, i_nb:i_nb + 1], in_=vecs[:, i_nb:i_nb + 1], mul=-1.0), reads=[vecs], writes=[vecs])
        S.op("dve", lambda e: e.tensor_scalar(out=vecs[:, i_om:i_om + 1], in0=vecs[:, i_om:i_om + 1], scalar1=-1.0, scalar2=1.0, op0=ALU.mult, op1=ALU.add), reads=[vecs], writes=[vecs])

        def V(name, lo=0, hi=128):
            i = P2V[name]
            return vecs[lo:hi, i:i + 1]

        I128 = lambda: cst[:, 0:128]
        I64 = lambda lo, hi: cst[lo:hi, 128:192]
        BO = lambda: cst[:, 192:320]
        MK1 = cst[:, 320:448]
        MK3 = cst[:, 448:512]
        MKG = lambda: cst[:, 512:640]
        R64 = lambda: cst[:, 640:640 + TB]
        R128 = lambda: cst[:, 640 + TB:640 + 2 * TB]

        Sg = S.sb("Sg", [64, 128])
        Sr = S.sb("Sr", [128, 64])
        S.op("pool", lambda e: e.memset(Sg[:, :], 0.0), writes=[Sg])
        S.op("pool", lambda e: e.memset(Sr[:, :], 0.0), writes=[Sr])

        def W(name, shape=None):
            return S.sb(name, shape or [128, TB])

        zb = {nm: S.sb("z_" + nm, [128, TB + 1]) for nm in ("rr", "rk", "rv", "rw", "ra", "rg0", "rg1", "vr")}
        zrow = {"rr": 5, "rk": 6, "rv": 7, "rw": 8, "ra": 9, "rg0": 10, "rg1": 11, "vr": 12}
        zmu = {"rr": "mu_r", "rk": "mu_k", "rv": "mu_v", "rw": "mu_w", "ra": "mu_a", "rg0": "mu_g0", "rg1": "mu_g1", "vr": "mu_vr"}
        gq, gk, gv, gg, ga = W("gq"), W("gk"), W("gv"), W("gg"), W("ga")
        tmpA, tmpB, tmpC = W("tmpA"), W("tmpB"), W("tmpC")
        r_, k_, v_, a_, kkn = W("r_"), W("k_"), W("v_"), W("a_"), W("kkn")
        ld, cl, E1, E2, E3 = W("ld"), W("cl"), W("E1"), W("E2"), W("E3")
        gate, bonus = W("gate"), W("bonus")
        AR = W("AR", [128, NCk, 2, 64])
        BT, KT, BH, KH = W("BT"), W("KT"), W("BH"), W("KH")
        TMV, TMB, TMK = W("TMV", [128, NCk, 64]), W("TMB", [128, NCk, 64]), W("TMK", [128, NCk, 64])
        SB1, SB2 = W("SB1", [128, NCk, 128]), W("SB2", [128, NCk, 128])
        A0 = W("A0", [128, NCk, 64])
        BG = W("BG", [128, NCk, 128])
        Xs = [S.sb("X%d" % i, [128, 64]) for i in range(2)]
        Us = [S.sb("U%d" % i, [128, 64]) for i in range(2)]
        Y = W("Y", [128, NCk, 64])
        st1, st2, st3 = S.sb("st1", [128, NCk]), S.sb("st2", [128, NCk]), S.sb("st3", [128, NCk])
        gVt = W("gVt", [128, NG, 128])
        gKt = W("gKt", [128, NG, 64])
        gST = [S.sb("gST%d" % i, [128, 128]) for i in range(2)]
        gO = W("gO", [128, NG, 128])
        gs1, gs2 = S.sb("gs1", [128, NG]), S.sb("gs2", [128, NG])
        outb = W("outb")

        def mm(out, lhsT, rhs, reads, writes, start=True, stop=True):
            S.op("pe", lambda e: e.matmul(out, lhsT, rhs, start=start, stop=stop), reads=reads, writes=writes, signal=stop)

        def act(out, in_, func, reads, writes, scale=None, bias=None):
            kw = {}
            if scale is not None:
                kw["scale"] = scale
            if bias is not None:
                kw["bias"] = bias
            S.op("act", lambda e: e.activation(out=out, in_=in_, func=func, **kw), reads=reads, writes=writes)

        def tt(eng, out, in0, in1, op, reads, writes):
            S.op(eng, lambda e: e.tensor_tensor(out=out, in0=in0, in1=in1, op=op), reads=reads, writes=writes)

        def ts(eng, out, in0, s1, s2, op0, op1, reads, writes):
            S.op(eng, lambda e: e.tensor_scalar(out=out, in0=in0, scalar1=s1, scalar2=s2, op0=op0, op1=op1), reads=reads, writes=writes)

        def stt(out, in0, scalar, in1, op0, op1, reads, writes):
            S.op("dve", lambda e: e.scalar_tensor_tensor(out=out, in0=in0, scalar=scalar, in1=in1, op0=op0, op1=op1), reads=reads, writes=writes)

        def mm_full(dst, lhs_fn, rhs_fn, nk, reads, post, m=128):
            for h in range(NH):
                pb = PP.get()
                for k in range(nk):
                    mm(pb[0:m, 0:HB], lhs_fn(k), rhs_fn(k, h), reads, [pb], start=(k == 0), stop=(k == nk - 1))
                post(pb, h)

        for tb in range(NB):
            t0 = tb * TB
            for nm, tl in zb.items():
                if nm == "vr" and layer == 0:
                    continue
                r0 = zrow[nm] * 128
                if tb == 0:
                    S.op("pool", lambda e, tl=tl: e.memset(tl[:, 0:1], 0.0), writes=[tl])
                    S.dma(lambda e, tl=tl, r0=r0: e.dma_start(out=tl[:, 1:TB + 1], in_=zin.ap()[r0:r0 + 128, 0:TB]), writes=[tl])
                else:
                    S.dma(lambda e, tl=tl, r0=r0, t0=t0: e.dma_start(out=tl[:, 0:TB + 1], in_=zin.ap()[r0:r0 + 128, t0 - 1:t0 + TB]), writes=[tl])
            for i, tl in enumerate((gq, gk, gv, gg, ga)):
                S.dma(lambda e, tl=tl, i=i, t0=t0: e.dma_start(out=tl[:, :], in_=zin.ap()[i * 128:(i + 1) * 128, t0:t0 + TB]), writes=[tl])
            if DBG in zb and tb == 0:
                S.dma(lambda e: e.dma_start(out=dbg_d.ap(), in_=zb[DBG][:, :]), reads=[zb[DBG]], out=True)
            lz = {}
            for nm, tl in zb.items():
                if nm == "vr" and layer == 0:
                    continue
                tt("pool", tmpA[:, :], tl[:, 0:TB], tl[:, 1:TB + 1], ALU.subtract, [tl], [tmpA])
                dst = {"rr": r_, "rk": k_, "rv": v_}.get(nm)
                if dst is None:
                    dst = tl
                    stt(tl[:, 1:TB + 1], tmpA[:, :], V(zmu[nm]), tl[:, 1:TB + 1], ALU.mult, ALU.add, [tmpA, vecs, tl], [tl])
                    lz[nm] = (tl, lambda tl=tl: tl[:, 1:TB + 1])
                else:
                    stt(dst[:, :], tmpA[:, :], V(zmu[nm]), tl[:, 1:TB + 1], ALU.mult, ALU.add, [tmpA, vecs, tl], [dst])
            if DBG == "v_" and tb == 0:
                S.dma(lambda e: e.dma_start(out=dbg_d.ap()[:, 0:TB], in_=v_[:, :]), reads=[v_], out=True)
            if DBG == "tmpA" and tb == 0:
                S.dma(lambda e: e.dma_start(out=dbg_d.ap()[:, 0:TB], in_=tmpA[:, :]), reads=[tmpA], out=True)
            rw_t, rw = lz["rw"]
            act(rw(), rw(), AF.Tanh, [rw_t], [rw_t])
            for nm in ("rg0", "rg1"):
                g_t, g_ = lz[nm]
                act(g_(), g_(), AF.Sigmoid, [g_t], [g_t])
            ra_t, ra = lz["ra"]

            if layer > 0:
                vr_t, vr = lz["vr"]
                vf = tmpC
                S.dma(lambda e, t0=t0: e.dma_start(out=vf[:, :], in_=vfi.ap()[:, t0:t0 + TB]), writes=[vf])

                def post_v(pb, h):
                    sl = slice(h * HB, (h + 1) * HB)
                    act(tmpB[:, sl], pb[:, 0:HB], AF.Sigmoid, [pb, vecs], [tmpB], bias=V("v0"))
                mm_full(None, lambda k: lora[:, 6, :], lambda k, h: vr_t[:, 1 + h * HB:1 + (h + 1) * HB], 1, [lora, vr_t], post_v)
                tt("pool", tmpC[:, :], tmpC[:, :], v_[:, :], ALU.subtract, [tmpC, v_], [tmpC])
                tt("dve", tmpC[:, :], tmpC[:, :], tmpB[:, :], ALU.mult, [tmpC, tmpB], [tmpC])
                tt("pool", v_[:, :], v_[:, :], tmpC[:, :], ALU.add, [tmpC, v_], [v_])
            else:
                S.dma(lambda e, t0=t0: e.dma_start(out=vfo.ap()[:, t0:t0 + TB], in_=v_[:, :]), reads=[v_], out=True)

            def post_ld(pb, h):
                sl = slice(h * HB, (h + 1) * HB)
                act(ld[:, sl], pb[:, 0:HB], AF.Sigmoid, [pb, vecs], [ld], bias=V("w0"))
            mm_full(None, lambda k: lora[:, 1, :], lambda k, h: rw_t[:, 1 + h * HB:1 + (h + 1) * HB], 1, [lora, rw_t], post_ld)
            S.op("pool", lambda e: e.tensor_scalar(out=ld[:, :], in0=ld[:, :], scalar1=LD_SCALE, scalar2=None, op0=ALU.mult), reads=[ld], writes=[ld])

            def post_a(pb, h):
                sl = slice(h * HB, (h + 1) * HB)
                act(a_[:, sl], pb[:, 0:HB], AF.Sigmoid, [pb, vecs], [a_], bias=V("a0"))
            mm_full(None, lambda k: lora[:, 2, :], lambda k, h: ra_t[:, 1 + h * HB:1 + (h + 1) * HB], 1, [lora, ra_t], post_a)

            def post_g(pb, h):
                sl = slice(h * HB, (h + 1) * HB)
                S.ev(lambda e: e.copy(out=gate[:, sl], in_=pb[:, 0:HB]), lambda e: e.tensor_copy(out=gate[:, sl], in_=pb[:, 0:HB]), [pb], [gate])
            g0_t, _ = lz["rg0"]
            g1_t, _ = lz["rg1"]
            mm_full(None, lambda k: lora[:, 3 + k, :], lambda k, h: (g0_t, g1_t)[k][:, 1 + h * HB:1 + (h + 1) * HB], 2, [lora, g0_t, g1_t], post_g)

            S.op("pool", lambda e: e.tensor_scalar(out=kkn[:, :], in0=k_[:, :], scalar1=V("k_k"), scalar2=None, op0=ALU.mult), reads=[k_, vecs], writes=[kkn])
            tt("dve", tmpA[:, :], kkn[:, :], kkn[:, :], ALU.mult, [kkn], [tmpA])

            def post_kk(pb, h):
                sl = slice(h * HB, (h + 1) * HB)
                S.op("dve", lambda e: e.tensor_scalar(out=tmpB[:, sl], in0=pb[:, 0:HB], scalar1=1e-24, scalar2=None, op0=ALU.max), reads=[pb], writes=[tmpB])
            mm_full(None, lambda k: BO(), lambda k, h: tmpA[:, h * HB:(h + 1) * HB], 1, [cst, tmpA], post_kk)
            act(tmpB[:, :], tmpB[:, :], AF.Sqrt, [tmpB], [tmpB])
            S.op("dve", lambda e: e.reciprocal(tmpB[:, :], tmpB[:, :]), reads=[tmpB], writes=[tmpB])
            tt("pool", kkn[:, :], kkn[:, :], tmpB[:, :], ALU.mult, [kkn, tmpB], [kkn])
            S.op("dve", lambda e: e.tensor_scalar(out=tmpA[:, :], in0=a_[:, :], scalar1=V("k_a"), scalar2=V("omka"), op0=ALU.mult, op1=ALU.add), reads=[a_, vecs], writes=[tmpA])
            tt("pool", k_[:, :], k_[:, :], tmpA[:, :], ALU.mult, [k_, tmpA], [k_])
            stt(tmpA[:, :], r_[:, :], V("r_k"), k_[:, :], ALU.mult, ALU.mult, [r_, k_, vecs], [tmpA])

            def post_bonus(pb, h):
                sl = slice(h * HB, (h + 1) * HB)
                tt("dve", bonus[:, sl], pb[:, 0:HB], v_[:, sl], ALU.mult, [pb, v_], [bonus])
            mm_full(None, lambda k: BO(), lambda k, h: tmpA[:, h * HB:(h + 1) * HB], 1, [cst, tmpA], post_bonus)

            S.op("dve", lambda e: e.tensor_tensor_scan(out=cl[:, :], data0=R64(), data1=ld[:, :], initial=0.0, op0=ALU.mult, op1=ALU.add), reads=[cst, ld], writes=[cl])
            act(E1[:, :], cl[:, :], AF.Exp, [cl], [E1])
            act(E2[:, :], cl[:, :], AF.Exp, [cl], [E2], scale=-1.0)
            tt("pool", tmpA[:, :], cl[:, :], ld[:, :], ALU.subtract, [cl, ld], [tmpA])
            act(E3[:, :], tmpA[:, :], AF.Exp, [tmpA], [E3])
            ARv = AR[:, :, :, :]
            c3 = lambda tl: tl[:, :].rearrange("p (c t) -> p c t", t=64)
            stt(AR[:, :, 0, :], c3(kkn), -1.0, c3(E3), ALU.mult, ALU.mult, [kkn, E3], [AR])
            tt("pool", AR[:, :, 1, :], c3(r_), c3(E1), ALU.mult, [r_, E1], [AR])
            tt("pool", BT[:, :], kkn[:, :], a_[:, :], ALU.mult, [kkn, a_], [BT])
            tt("dve", BT[:, :], BT[:, :], E2[:, :], ALU.mult, [BT, E2], [BT])
            tt("pool", KT[:, :], k_[:, :], E2[:, :], ALU.mult, [k_, E2], [KT])
            e1v = E1[:, :]
            gC_b = bcast_ap(E1[:, 63:64], [[64, NCk], [0, 64]])
            tt("dve", c3(BH), c3(BT), gC_b, ALU.mult, [BT, E1], [BH])
            tt("pool", c3(KH), c3(KT), gC_b, ALU.mult, [KT, E1], [KH])

            for src, dst in ((v_, TMV), (BH, TMB), (KH, TMK)):
                for g in range(0, NCk, 8):
                    pb = PP.get()
                    n = min(8, NCk - g)
                    for c in range(g, g + n):
                        for p in range(2):
                            lo, hi = p * 64, p * 64 + 64
                            mm(pb[lo:hi, (c - g) * 64:(c - g + 1) * 64], src[lo:hi, c * 64:(c + 1) * 64], I64(lo, hi), [src, cst], [pb])
                    S.ev(lambda e, pb=pb, dst=dst, g=g, n=n: e.copy(out=dst[:, g:g + n, :], in_=pb[:, 0:n * 64].rearrange("p (c t) -> p c t", t=64)),
                         lambda e, pb=pb, dst=dst, g=g, n=n: e.tensor_copy(out=dst[:, g:g + n, :], in_=pb[:, 0:n * 64].rearrange("p (c t) -> p c t", t=64)),
                         [pb], [dst])

            mk1_b = lambda n: bcast_ap(MK1, [[0, n], [1, 128]])
            mk3_b = lambda n: bcast_ap(MK3, [[0, n], [1, 64]])
            for g in range(0, NCk, 4):
                n = min(4, NCk - g)
                for stat, dstt in ((BT, SB1), (KT, SB2)):
                    pb = PP.get()
                    for c in range(g, g + n):
                        for p in range(2):
                            lo, hi = p * 64, p * 64 + 64
                            mm(pb[lo:hi, (c - g) * 128:(c - g + 1) * 128], stat[lo:hi, c * 64:(c + 1) * 64], AR[lo:hi, c, :, :], [stat, AR], [pb])
                    tt("dve", dstt[:, g:g + n, :], pb[:, 0:n * 128].rearrange("p (c t) -> p c t", t=128), mk1_b(n), ALU.mult, [pb, cst], [dstt])
            for g in range(0, NCk, 8):
                n = min(8, NCk - g)
                pb = PP.get()
                for c in range(g, g + n):
                    for p in range(2):
                        lo, hi = p * 64, p * 64 + 64
                        mm(pb[lo:hi, (c - g) * 64:(c - g + 1) * 64], AR[lo:hi, c, 0, :], BT[lo:hi, c * 64:(c + 1) * 64], [AR, BT], [pb])
                tt("dve", A0[:, g:g + n, :], pb[:, 0:n * 64].rearrange("p (c t) -> p c t", t=64), mk3_b(n), ALU.mult, [pb, cst], [A0])

            S.op("pool", lambda e: e.tensor_copy(out=BG[:, :, 0:64], in_=SB1[:, :, 0:64]), reads=[SB1], writes=[BG])
            tt("pool", BG[:, :, 64:128], SB1[:, :, 0:64], bcast_ap(cst[:, 128:192], [[0, NCk], [1, 64]]), ALU.add, [SB1, cst], [BG])
            for kstep in range(6):
                last = (kstep == 5)
                first = (kstep == 0)
                qsl = slice(0, 64) if first else (slice(64, 128) if last else slice(0, 128))
                qw = 64 if (first or last) else 128
                per = 512 // qw
                newA = []
                for g in range(0, NCk, per):
                    n = min(per, NCk - g)
                    pb = PP.get()
                    for c in range(g, g + n):
                        for p in range(2):
                            lo, hi = p * 64, p * 64 + 64
                            mm(pb[lo:hi, (c - g) * qw:(c - g + 1) * qw], A0[lo:hi, c, :], BG[lo:hi, c, qsl], [A0, BG], [pb])
                    newA.append((pb, g, n))
                newR = []
                if not last:
                    for g in range(0, NCk, 8):
                        n = min(8, NCk - g)
                        pr = PP.get()
                        for c in range(g, g + n):
                            for p in range(2):
                                lo, hi = p * 64, p * 64 + 64
                                mm(pr[lo:hi, (c - g) * 64:(c - g + 1) * 64], BG[lo:hi, c, 0:64], A0[lo:hi, c, :], [A0, BG], [pr])
                        newR.append((pr, g, n))
                for pb, g, n in newA:
                    v3 = pb[:, 0:n * qw].rearrange("p (c t) -> p c t", t=qw)
                    if first:
                        S.ev(lambda e, v3=v3, g=g, n=n: e.copy(out=BG[:, g:g + n, 0:64], in_=v3), lambda e, v3=v3, g=g, n=n: e.tensor_copy(out=BG[:, g:g + n, 0:64], in_=v3), [pb], [BG])
                    elif last:
                        tt("dve", BG[:, g:g + n, 64:128], BG[:, g:g + n, 64:128], v3, ALU.add, [pb, BG], [BG])
                    else:
                        tt("dve", BG[:, g:g + n, 64:128], BG[:, g:g + n, 64:128], v3[:, :, 64:128], ALU.add, [pb, BG], [BG])
                        S.op("act", lambda e, v3=v3, g=g, n=n: e.copy(out=BG[:, g:g + n, 0:64], in_=v3[:, :, 0:64]), reads=[pb], writes=[BG])
                for pr, g, n in newR:
                    v3 = pr[:, 0:n * 64].rearrange("p (c t) -> p c t", t=64)
                    S.ev(lambda e, v3=v3, g=g, n=n: e.copy(out=A0[:, g:g + n, :], in_=v3), lambda e, v3=v3, g=g, n=n: e.tensor_copy(out=A0[:, g:g + n, :], in_=v3), [pr], [A0])

            for c in range(NCk):
                X, U = Xs[c % 2], Us[c % 2]
                px = PP.get()
                for p in range(2):
                    lo, hi = p * 64, p * 64 + 64
                    mm(px[lo:hi, 0:64], AR[lo:hi, c, 0, :], Sr[lo:hi, :], [AR, Sr], [px], start=True, stop=False)
                    mm(px[lo:hi, 0:64], SB2[lo:hi, c, 0:64], TMV[lo:hi, c, :], [SB2, TMV], [px], start=False, stop=True)
                S.ev(lambda e, px=px, X=X: e.copy(out=X[:, :], in_=px[:, 0:64]), lambda e, px=px, X=X: e.tensor_copy(out=X[:, :], in_=px[:, 0:64]), [px], [X])
                pu = PP.get()
                for p in range(2):
                    lo, hi = p * 64, p * 64 + 64
                    mm(pu[lo:hi, 0:64], BG[lo:hi, c, 64:128], X[lo:hi, :], [BG, X], [pu])
                S.ev(lambda e, pu=pu, U=U: e.copy(out=U[:, :], in_=pu[:, 0:64]), lambda e, pu=pu, U=U: e.tensor_copy(out=U[:, :], in_=pu[:, 0:64]), [pu], [U])
                py = PP.get()
                for p in range(2):
                    lo, hi = p * 64, p * 64 + 64
                    mm(py[lo:hi, 0:64], AR[lo:hi, c, 1, :], Sr[lo:hi, :], [AR, Sr], [py], start=True, stop=False)
                    mm(py[lo:hi, 0:64], SB1[lo:hi, c, 64:128], U[lo:hi, :], [SB1, U], [py], start=False, stop=False)
                    mm(py[lo:hi, 0:64], SB2[lo:hi, c, 64:128], TMV[lo:hi, c, :], [SB2, TMV], [py], start=False, stop=True)
                S.op("act", lambda e, py=py, c=c: e.copy(out=Y[:, c, :], in_=py[:, 0:64]), reads=[py], writes=[Y])
                pS = PP.get()
                for p in range(2):
                    lo, hi = p * 64, p * 64 + 64
                    mm(pS[lo:hi, 0:64], TMB[lo:hi, c, :], U[lo:hi, :], [TMB, U], [pS], start=True, stop=False)
                    mm(pS[lo:hi, 0:64], TMK[lo:hi, c, :], TMV[lo:hi, c, :], [TMK, TMV], [pS], start=False, stop=True)
                stt(Sr[:, :], Sr[:, :], E1[:, c * 64 + 63:c * 64 + 64], pS[:, 0:64], ALU.mult, ALU.add, [Sr, E1, pS], [Sr])

            S.op("dve", lambda e: e.tensor_reduce(out=st1[:, :], in_=Y[:, :, :], axis=AX.X, op=ALU.add), reads=[Y], writes=[st1])
            act(tmpA[:, 0:NCk * 64], Y[:, :, :].rearrange("p c t -> p (c t)"), AF.Square, [Y], [tmpA])
            S.op("dve", lambda e: e.tensor_reduce(out=st2[:, :], in_=tmpA[:, 0:NCk * 64].rearrange("p (c t) -> p c t", t=64), axis=AX.X, op=ALU.add), reads=[tmpA], writes=[st2])
            S.op("dve", lambda e: e.tensor_scalar(out=st1[:, :], in0=st1[:, :], scalar1=1.0 / 64, scalar2=None, op0=ALU.mult), reads=[st1], writes=[st1])
            tt("dve", st3[:, :], st1[:, :], st1[:, :], ALU.mult, [st1], [st3])
            stt(st2[:, :], st2[:, :], 1.0 / 64, st3[:, :], ALU.mult, ALU.subtract, [st2, st3], [st2])
            act(st2[:, :], st2[:, :], AF.Sqrt, [st2], [st2], bias=GN_EPS)
            S.op("dve", lambda e: e.reciprocal(st2[:, :], st2[:, :]), reads=[st2], writes=[st2])
            tt("dve", Y[:, :, :], Y[:, :, :], bcast_ap(st1[:, 0:1], [[1, NCk], [0, 64]]), ALU.subtract, [Y, st1], [Y])
            tt("dve", Y[:, :, :], Y[:, :, :], bcast_ap(st2[:, 0:1], [[1, NCk], [0, 64]]), ALU.mult, [Y, st2], [Y])
            for g in range(0, NCk, 8):
                n = min(8, NCk - g)
                pb = PP.get()
                for c in range(g, g + n):
                    for p in range(2):
                        lo, hi = p * 64, p * 64 + 64
                        mm(pb[lo:hi, (c - g) * 64:(c - g + 1) * 64], Y[lo:hi, c, :], I64(lo, hi), [Y, cst], [pb])
                sl = slice(g * 64, (g + n) * 64)
                S.op("dve", lambda e, pb=pb, sl=sl, n=n: e.tensor_scalar(out=outb[:, sl], in0=pb[:, 0:n * 64], scalar1=V("gn_w"), scalar2=V("gn_b"), op0=ALU.mult, op1=ALU.add), reads=[pb, vecs], writes=[outb])
            tt("pool", outb[:, :], outb[:, :], bonus[:, :], ALU.add, [outb, bonus], [outb])
            tt("pool", outb[:, :], outb[:, :], gate[:, :], ALU.mult, [outb, gate], [outb])
            S.dma(lambda e, t0=t0: e.dma_start(out=orw_d.ap()[:, t0:t0 + TB], in_=outb[:, :]), reads=[outb], out=True)

            def post_la(pb, h):
                sl = slice(h * HB, (h + 1) * HB)
                act(tmpA[0:64, sl], pb[0:64, 0:HB], AF.Exp, [pb, vecs], [tmpA], scale=-1.0, bias=V("nb_a", 0, 64))
            mm_full(None, lambda k: lora[:, 0, 0:64], lambda k, h: ga[:, h * HB:(h + 1) * HB], 1, [lora, ga], post_la, m=64)
            act(tmpA[0:64, :], tmpA[0:64, :], AF.Ln, [tmpA], [tmpA], bias=1.0)
            S.op("dve", lambda e: e.tensor_tensor_scan(out=tmpB[0:64, :], data0=cst[0:64, 640 + TB:640 + 2 * TB], data1=tmpA[0:64, :], initial=0.0, op0=ALU.mult, op1=ALU.add), reads=[cst, tmpA], writes=[tmpB])
            act(E1[0:64, :], tmpB[0:64, :], AF.Exp, [tmpB], [E1], scale=-1.0 / 16)
            act(E2[0:64, :], tmpB[0:64, :], AF.Exp, [tmpB], [E2], scale=1.0 / 16)
            stt(gq[0:64, :], gq[0:64, :], 0.125, E1[0:64, :], ALU.mult, ALU.mult, [gq, E1], [gq])
            tt("pool", gk[0:64, :], gk[0:64, :], E2[0:64, :], ALU.mult, [gk, E2], [gk])
            c128 = lambda ap: ap.rearrange("p (c t) -> p c t", t=128)
            gCg = bcast_ap(E1[0:64, 127:128], [[128, NG], [0, 128]])
            tt("dve", c128(tmpC[0:64, :]), c128(gk[0:64, :]), gCg, ALU.mult, [gk, E1], [tmpC])
            act(gg[:, :], gg[:, :], AF.Silu, [gg], [gg])
            for g in range(0, NG, 4):
                n = min(4, NG - g)
                pb = PP.get()
                for c in range(g, g + n):
                    mm(pb[:, (c - g) * 128:(c - g + 1) * 128], gv[:, c * 128:(c + 1) * 128], I128(), [gv, cst], [pb])
                S.ev(lambda e, pb=pb, g=g, n=n: e.copy(out=gVt[:, g:g + n, :], in_=pb[:, 0:n * 128].rearrange("p (c t) -> p c t", t=128)),
                     lambda e, pb=pb, g=g, n=n: e.tensor_copy(out=gVt[:, g:g + n, :], in_=pb[:, 0:n * 128].rearrange("p (c t) -> p c t", t=128)), [pb], [gVt])
            for g in range(0, NG, 8):
                n = min(8, NG - g)
                pb = PP.get()
                for c in range(g, g + n):
                    mm(pb[:, (c - g) * 64:(c - g + 1) * 64], tmpC[0:64, c * 128:(c + 1) * 128], I64(0, 64), [tmpC, cst], [pb])
                S.ev(lambda e, pb=pb, g=g, n=n: e.copy(out=gKt[:, g:g + n, :], in_=pb[:, 0:n * 64].rearrange("p (c t) -> p c t", t=64)),
                     lambda e, pb=pb, g=g, n=n: e.tensor_copy(out=gKt[:, g:g + n, :], in_=pb[:, 0:n * 64].rearrange("p (c t) -> p c t", t=64)), [pb], [gKt])
            for c in range(NG):
                cs = slice(c * 128, (c + 1) * 128)
                STm = gST[c % 2]
                pb = PP.get()
                mm(pb[:, 0:128], gk[0:64, cs], gq[0:64, cs], [gk, gq], [pb])
                tt("dve", STm[:, :], pb[:, 0:128], MKG(), ALU.mult, [pb, cst], [STm])
                po = PP.get()
                mm(po[:, 0:128], gq[0:64, cs], Sg[:, :], [gq, Sg], [po], start=True, stop=False)
                mm(po[:, 0:128], STm[:, :], gVt[:, c, :], [STm, gVt], [po], start=False, stop=True)
                S.op("act", lambda e, po=po, c=c: e.copy(out=gO[:, c, :], in_=po[:, 0:128]), reads=[po], writes=[gO])
                pS = PP.get()
                mm(pS[0:64, 0:128], gKt[:, c, :], gVt[:, c, :], [gKt, gVt], [pS])
                stt(Sg[:, :], Sg[:, :], E1[0:64, c * 128 + 127:c * 128 + 128], pS[0:64, 0:128], ALU.mult, ALU.add, [Sg, E1, pS], [Sg])
            act(tmpA[:, 0:NG * 128], gO[:, :, :].rearrange("p c t -> p (c t)"), AF.Square, [gO], [tmpA])
            S.op("dve", lambda e: e.tensor_reduce(out=gs1[:, :], in_=tmpA[:, 0:NG * 128].rearrange("p (c t) -> p c t", t=128), axis=AX.X, op=ALU.add), reads=[tmpA], writes=[gs1])
            act(gs1[:, :], gs1[:, :], AF.Sqrt, [gs1], [gs1], scale=1.0 / 128, bias=EPS)
            S.op("dve", lambda e: e.reciprocal(gs1[:, :], gs1[:, :]), reads=[gs1], writes=[gs1])
            tt("dve", gO[:, :, :], gO[:, :, :], bcast_ap(gs1[:, 0:1], [[1, NG], [0, 128]]), ALU.mult, [gO, gs1], [gO])
            for g in range(0, NG, 4):
                n = min(4, NG - g)
                pb = PP.get()
                for c in range(g, g + n):
                    mm(pb[:, (c - g) * 128:(c - g + 1) * 128], gO[:, c, :], I128(), [gO, cst], [pb])
                sl = slice(g * 128, (g + n) * 128)
                stt(outb[:, sl], pb[:, 0:n * 128], V("gnorm_w"), gg[:, sl], ALU.mult, ALU.mult, [pb, vecs, gg], [outb])
            S.dma(lambda e, t0=t0: e.dma_start(out=og_d.ap()[:, t0:t0 + TB], in_=outb[:, :]), reads=[outb], out=True)

        S.finish()
        S.emit()
    return nc


NZ = 54
ZQ, ZK, ZV, ZG, ZA, RR, RK, RV, RWc, RAc, RGc, VRc = 0, 4, 8, 16, 24, 25, 33, 41, 49, 50, 51, 53
GLA_COLS = 3088


def p1_colmap():
    m = -np.ones(NZ * 128, np.int64)

    def put(chunk, cols):
        m[chunk * 128: chunk * 128 + len(cols)] = cols
    put(ZQ, np.arange(0, 512)); put(ZK, np.arange(512, 1024)); put(ZV, np.arange(1024, 2048)); put(ZG, np.arange(2048, 3072))
    put(ZA, np.arange(3072, 3088))
    b = GLA_COLS
    put(RR, b + np.arange(0, 1024)); put(RK, b + np.arange(1024, 2048)); put(RV, b + np.arange(2048, 3072))
    put(RWc, b + np.arange(3072, 3168)); put(RAc, b + np.arange(3168, 3264)); put(RGc, b + np.arange(3264, 3520))
    put(VRc, 6608 + np.arange(64))
    return m


def z_to_p1_layout(z, vr):
    full = np.concatenate([z, vr], 1)
    m = p1_colmap()
    out = np.zeros((NZ * 128, z.shape[0]), np.float32)
    out[m >= 0] = full[:, m[m >= 0]].T
    return out


def p2_rowsel(j):
    rows = -np.ones(13 * 128, np.int64)

    def put(b, src, n=128):
        rows[b * 128: b * 128 + n] = src + np.arange(n)
    put(0, (ZQ + j // 2) * 128 + (j % 2) * 64, 64)
    put(1, (ZK + j // 2) * 128 + (j % 2) * 64, 64)
    put(2, (ZV + j) * 128); put(3, (ZG + j) * 128); put(4, ZA * 128)
    put(5, (RR + j) * 128); put(6, (RK + j) * 128); put(7, (RV + j) * 128)
    put(8, RWc * 128); put(9, RAc * 128); put(10, RGc * 128); put(11, (RGc + 1) * 128); put(12, VRc * 128)
    return rows


def p2_inputs(zT, P, l, j, TB, vfin=None):
    rows = p2_rowsel(j)
    zin = np.zeros((13 * 128, zT.shape[1]), np.float32)
    zin[rows >= 0] = zT[rows[rows >= 0]]
    c = slice(j * 128, (j + 1) * 128)
    mu = np.asarray(P["rwkv_mu"][l])

    def pad(v):
        o = np.zeros(128, np.float32)
        o[: len(v)] = v
        return o
    vv = {
        "nb_a": pad(np.asarray(P["gla_b_a"][l])[j * 64:(j + 1) * 64]), "gnorm_w": np.asarray(P["gla_norm_w"][l])[c],
        "mu_r": mu[0:1024][c], "mu_k": mu[1024:2048][c], "mu_v": mu[2048:3072][c], "mu_w": pad(mu[3072:3168]), "mu_a": pad(mu[3168:3264]),
        "mu_g0": mu[3264:3392], "mu_g1": mu[3392:3520], "mu_vr": pad(np.asarray(P["vres_mu"][0])),
        "w0": np.asarray(P["rwkv_w0"][l])[c], "a0": np.asarray(P["rwkv_a0"][l])[c], "k_k": np.asarray(P["rwkv_k_k"][l])[c],
        "k_a": np.asarray(P["rwkv_k_a"][l])[c], "omka": np.asarray(P["rwkv_k_a"][l])[c], "r_k": np.asarray(P["rwkv_r_k"][l]).reshape(-1)[c],
        "gn_w": np.asarray(P["rwkv_gn_w"][l])[c], "gn_b": np.asarray(P["rwkv_gn_b"][l])[c], "v0": np.asarray(P["vres_v0"][0])[c],
    }
    vecs = np.ascontiguousarray(np.stack([vv[n] for n in P2_VECS], 1).astype(np.float32))
    lora = np.zeros((128, 7, 128), np.float32)
    lora[0:16, 0, 0:64] = np.asarray(P["gla_w_a_up"][l])[:, j * 64:(j + 1) * 64]
    lora[0:96, 1, :] = np.asarray(P["rwkv_w_up"][l])[:, c]
    lora[0:96, 2, :] = np.asarray(P["rwkv_a_up"][l])[:, c]
    lora[:, 3, :] = np.asarray(P["rwkv_g_up"][l])[0:128, c]
    lora[:, 4, :] = np.asarray(P["rwkv_g_up"][l])[128:256, c]
    lora[0:64, 6, :] = np.asarray(P["vres_up"][0])[:, c]
    d = {"zin": zin, "vecs": vecs, "cst": p2_consts(TB), "lora": lora}
    if vfin is not None:
        d["vfin"] = np.ascontiguousarray(vfin[c], dtype=np.float32)
    return d


class Ctx:
    pass


def _helpers(S):
    C = Ctx()

    def mm(out, lhsT, rhs, reads, writes, start=True, stop=True):
        S.op("pe", lambda e: e.matmul(out, lhsT, rhs, start=start, stop=stop), reads=reads, writes=writes, signal=stop)

    def act(out, in_, func, reads, writes, scale=None, bias=None):
        kw = {}
        if scale is not None:
            kw["scale"] = scale
        if bias is not None:
            kw["bias"] = bias
        S.op("act", lambda e: e.activation(out=out, in_=in_, func=func, **kw), reads=reads, writes=writes)

    def tt(eng, out, in0, in1, op, reads, writes):
        S.op(eng, lambda e: e.tensor_tensor(out=out, in0=in0, in1=in1, op=op), reads=reads, writes=writes)

    def ts(eng, out, in0, s1, s2, op0, op1, reads, writes):
        if s2 is None:
            S.op(eng, lambda e: e.tensor_scalar(out=out, in0=in0, scalar1=s1, scalar2=None, op0=op0), reads=reads, writes=writes)
        else:
            S.op(eng, lambda e: e.tensor_scalar(out=out, in0=in0, scalar1=s1, scalar2=s2, op0=op0, op1=op1), reads=reads, writes=writes)

    def stt(out, in0, scalar, in1, op0, op1, reads, writes):
        S.op("dve", lambda e: e.scalar_tensor_tensor(out=out, in0=in0, scalar=scalar, in1=in1, op0=op0, op1=op1), reads=reads, writes=writes)

    def dma(out, in_, reads=(), writes=(), q="sp", isout=False):
        S.dma(lambda e: e.dma_start(out=out, in_=in_), reads=reads, writes=writes, q=q, out=isout)

    def recip(ap, tl):
        S.op("dve", lambda e: e.reciprocal(ap, ap), reads=[tl], writes=[tl])

    C.mm, C.act, C.tt, C.ts, C.stt, C.dma, C.recip = mm, act, tt, ts, stt, dma, recip
    return C


def emit_mod(S, C, PP, cT_d, wada_d, nchunks, bcol_fn, out_tile, cond, slabs):
    pm = PP.get()
    for n in range(nchunks):
        sl = slabs[n % len(slabs)]
        C.dma(sl[:, :, :], wada_d.ap()[n], writes=[sl])
        for k in range(KC):
            C.mm(pm[:, n:n + 1], sl[:, k, :], cond[:, k:k + 1], [sl, cond], [pm], start=(k == 0), stop=(k == KC - 1))
    C.tt("dve", out_tile[:, 0:nchunks], pm[:, 0:nchunks], bcol_fn(), ALU.add, [pm], [out_tile])


def emit_rstd(S, C, PP, src, rstd, ones, sq, TH, nk=KC):
    ps = PP.get()
    for k in range(nk):
        q = sq[k % 2]
        C.act(q[:, :], src[:, k, :], AF.Square, [src], [q])
        S.op("pe", lambda e, k=k, q=q: e.matmul(ps[:, 0:TH], ones[:, :], q[:, :], start=(k == 0), stop=(k == nk - 1)),
             reads=[ones, q], writes=[ps], signal=True)
    C.act(rstd[:, :], ps[:, 0:TH], AF.Sqrt, [ps], [rstd], scale=1.0 / D, bias=EPS)
    C.recip(rstd[:, :], rstd)


def build_p1(T, nz):
    TH = min(512, T)
    NH = T // TH
    nc = bass.Bass("TRN2", target_bir_lowering=False)
    x_d = nc.dram_tensor("xT", [D, T], F32, kind="ExternalInput")
    c_d = nc.dram_tensor("cT", [128, KC], F32, kind="ExternalInput")
    wada_d = nc.dram_tensor("wada", [32, 128, KC, 128], F32, kind="ExternalInput")
    vec_d = nc.dram_tensor("vecs", [128, 48], F32, kind="ExternalInput")
    win_d = nc.dram_tensor("winp", [nz, 128, KC, 128], F32, kind="ExternalInput")
    z_d = nc.dram_tensor("zT", [nz * 128, T], F32, kind="ExternalOutput")
    with contextlib.ExitStack() as st:
        S = Sched(nc, st)
        C = _helpers(S)
        PP = PsumPool(S, 8)
        vecs = S.sb("vecs", [128, 48])
        cond = S.sb("cond", [128, KC])
        ones = S.sb("ones", [128, 128])
        modA = S.sb("modA", [128, 32])
        gs = S.sb("gs", [128, KC])
        aslab = [S.sb("aslab%d" % i, [128, KC, 128]) for i in range(2)]
        xs = S.sb("xs", [128, KC, T])
        hT = S.sb("hT", [128, KC, T], BF16)
        sq = [S.sb("sq%d" % i, [128, TH]) for i in range(2)]
        rstd = [S.sb("rstd%d" % i, [128, TH]) for i in range(NH)]
        tmp = [S.sb("tmp%d" % i, [128, TH]) for i in range(2)]
        wsl = [S.sb("wsl%d" % i, [128, KC, 128], BF16) for i in range(4)]
        wst = [S.sb("wst%d" % i, [128, KC, 128]) for i in range(2)]
        zb = [S.sb("zb%d" % i, [128, T]) for i in range(3)]
        C.dma(vecs[:, :], vec_d.ap(), writes=[vecs])
        C.dma(cond[:, :], c_d.ap(), writes=[cond])
        S.op("dve", lambda e: e.memset(ones[:, :], 1.0), writes=[ones])
        for k in range(KC):
            C.dma(xs[:, k, :], x_d.ap()[k * 128:(k + 1) * 128, :], writes=[xs])
        C.act(cond[:, :], cond[:, :], AF.Silu, [cond], [cond])
        emit_mod(S, C, PP, c_d, wada_d, 32, lambda: vecs[:, 0:32], modA, cond, aslab)
        C.ts("dve", gs[:, :], modA[:, 16:32], 1.0, None, ALU.add, None, [modA], [gs])
        C.tt("dve", gs[:, :], gs[:, :], vecs[:, 32:48], ALU.mult, [gs, vecs], [gs])
        for h in range(NH):
            hs = slice(h * TH, (h + 1) * TH)
            xv = T_view(xs, lambda k, hs=hs: xs[:, k, hs])
            emit_rstd(S, C, PP, xv, rstd[h], ones, sq, TH)
            for k in range(KC):
                t_ = tmp[k % 2]
                C.stt(t_[:, :], xs[:, k, hs], gs[:, k:k + 1], rstd[h][:, :], ALU.mult, ALU.mult, [xs, gs, rstd[h]], [t_])
                C.act(hT[:, k, hs], t_[:, :], AF.Identity, [t_, modA], [hT], bias=modA[:, k:k + 1])
        for ci in range(nz):
            w = wsl[ci % 4]
            sg_ = wst[ci % 2]
            C.dma(sg_[:, :, :], win_d.ap()[ci], writes=[sg_])
            S.ev(lambda e, w=w, sg_=sg_: e.copy(out=w[:, :, :], in_=sg_[:, :, :]), lambda e, w=w, sg_=sg_: e.tensor_copy(out=w[:, :, :], in_=sg_[:, :, :]), [sg_], [w])
            zt = zb[ci % 3]
            for h in range(NH):
                hs = slice(h * TH, (h + 1) * TH)
                pb = PP.get()
                for k in range(KC):
                    C.mm(pb[:, 0:TH], w[:, k, :], hT[:, k, hs], [w, hT], [pb], start=(k == 0), stop=(k == KC - 1))
                S.ev(lambda e, pb=pb, zt=zt, hs=hs: e.copy(out=zt[:, hs], in_=pb[:, 0:TH]), lambda e, pb=pb, zt=zt, hs=hs: e.tensor_copy(out=zt[:, hs], in_=pb[:, 0:TH]), [pb], [zt])
            C.dma(z_d.ap()[ci * 128:(ci + 1) * 128, :], zt[:, :], reads=[zt], isout=True)
        S.finish()
        S.emit()
    return nc


class T_view:
    def __init__(self, base, fn):
        self.base = base
        self.fn = fn

    def __getitem__(self, idx):
        return self.fn(idx[1])

    @property
    def w(self):
        return self.base.w

    @w.setter
    def w(self, v):
        self.base.w = v

    @property
    def r(self):
        return self.base.r

    @r.setter
    def r(self, v):
        self.base.r = v


def build_p3(T):
    TH = min(256, T)
    NH = T // TH
    nc = bass.Bass("TRN2", target_bir_lowering=False)
    x_d = nc.dram_tensor("xT", [D, T], F32, kind="ExternalInput")
    o_d = nc.dram_tensor("oT", [D, T], F32, kind="ExternalInput")
    c_d = nc.dram_tensor("cT", [128, KC], F32, kind="ExternalInput")
    wada_d = nc.dram_tensor("wada", [64, 128, KC, 128], F32, kind="ExternalInput")
    vec_d = nc.dram_tensor("vecs", [128, 112], F32, kind="ExternalInput")
    wout_d = nc.dram_tensor("wout", [16, 128, KC, 128], F32, kind="ExternalInput")
    w1_d = nc.dram_tensor("w1", [64, 128, KC, 128], F32, kind="ExternalInput")
    w2_d = nc.dram_tensor("w2", [16, 128, 64, 128], F32, kind="ExternalInput")
    xo_d = nc.dram_tensor("xo", [D, T], F32, kind="ExternalOutput")
    with contextlib.ExitStack() as st:
        S = Sched(nc, st)
        C = _helpers(S)
        PP = PsumPool(S, 8)
        vecs = S.sb("vecs", [128, 112])
        cond = S.sb("cond", [128, KC])
        ones = S.sb("ones", [128, 128])
        modB = S.sb("modB", [128, 64])
        ggt1, gs2, ggt2 = S.sb("ggt1", [128, KC]), S.sb("gs2", [128, KC]), S.sb("ggt2", [128, KC])
        C.dma(vecs[:, :], vec_d.ap(), writes=[vecs])
        C.dma(cond[:, :], c_d.ap(), writes=[cond])
        S.op("dve", lambda e: e.memset(ones[:, :], 1.0), writes=[ones])
        C.act(cond[:, :], cond[:, :], AF.Silu, [cond], [cond])
        with contextlib.ExitStack() as st2:
            aslab = [S.sb("aslab%d" % i, [128, KC, 128], stack=st2) for i in range(2)]
            emit_mod(S, C, PP, c_d, wada_d, 64, lambda: vecs[:, 0:64], modB, cond, aslab)
            S.barrier()
        C.tt("dve", ggt1[:, :], modB[:, 0:16], vecs[:, 64:80], ALU.mult, [modB, vecs], [ggt1])
        C.ts("dve", gs2[:, :], modB[:, 32:48], 1.0, None, ALU.add, None, [modB], [gs2])
        C.tt("dve", gs2[:, :], gs2[:, :], vecs[:, 80:96], ALU.mult, [gs2, vecs], [gs2])
        C.tt("dve", ggt2[:, :], modB[:, 48:64], vecs[:, 96:112], ALU.mult, [modB, vecs], [ggt2])

        ob = S.sb("ob", [128, KC, TH], BF16)
        yT = S.sb("yT", [128, KC, TH])
        xs = S.sb("xs", [128, KC, TH])
        hT = S.sb("hT", [128, KC, TH], BF16)
        uT = S.sb("uT", [128, 64, TH], BF16)
        wsl = [S.sb("wsl%d" % i, [128, KC, 128], BF16) for i in range(3)]
        w2s = [S.sb("w2s%d" % i, [128, 32, 128], BF16) for i in range(2)]
        wst = [S.sb("wst%d" % i, [128, KC, 128]) for i in range(2)]
        w2st = [S.sb("w2st%d" % i, [128, 32, 128]) for i in range(2)]
        ost = [S.sb("ost%d" % i, [128, TH]) for i in range(2)]
        sq = [S.sb("sq%d" % i, [128, TH]) for i in range(2)]
        rstd = S.sb("rstd", [128, TH])
        wi = [0, 0]

        def wslab(src_ap):
            w = wsl[wi[0] % 3]
            wi[0] += 1
            sg_ = wst[wi[0] % 2]
            C.dma(sg_[:, :, :], src_ap, writes=[sg_])
            S.ev(lambda e, w=w, sg_=sg_: e.copy(out=w[:, :, :], in_=sg_[:, :, :]), lambda e, w=w, sg_=sg_: e.tensor_copy(out=w[:, :, :], in_=sg_[:, :, :]), [sg_], [w])
            return w

        def w2slab(src_ap):
            w = w2s[wi[1] % 2]
            wi[1] += 1
            sg_ = w2st[wi[1] % 2]
            C.dma(sg_[:, :, :], src_ap, writes=[sg_])
            S.ev(lambda e, w=w, sg_=sg_: e.copy(out=w[:, :, :], in_=sg_[:, :, :]), lambda e, w=w, sg_=sg_: e.tensor_copy(out=w[:, :, :], in_=sg_[:, :, :]), [sg_], [w])
            return w

        def resid(gg, isout):
            for k in range(KC):
                q = sq[k % 2]
                C.stt(q[:, :], yT[:, k, :], gg[:, k:k + 1], rstd[:, :], ALU.mult, ALU.mult, [yT, gg, rstd], [q])
                C.tt("dve", xs[:, k, :], xs[:, k, :], q[:, :], ALU.add, [xs, q], [xs])

        for h in range(NH):
            hs = slice(h * TH, (h + 1) * TH)
            for k in range(KC):
                og_ = ost[k % 2]
                C.dma(og_[:, :], o_d.ap()[k * 128:(k + 1) * 128, hs], writes=[og_])
                S.ev(lambda e, k=k, og_=og_: e.copy(out=ob[:, k, :], in_=og_[:, :]), lambda e, k=k, og_=og_: e.tensor_copy(out=ob[:, k, :], in_=og_[:, :]), [og_], [ob])
                C.dma(xs[:, k, :], x_d.ap()[k * 128:(k + 1) * 128, hs], writes=[xs])
            for m in range(16):
                w = wslab(wout_d.ap()[m])
                pb = PP.get()
                for k in range(KC):
                    C.mm(pb[:, 0:TH], w[:, k, :], ob[:, k, :], [w, ob], [pb], start=(k == 0), stop=(k == KC - 1))
                S.ev(lambda e, pb=pb, m=m: e.copy(out=yT[:, m, :], in_=pb[:, 0:TH]), lambda e, pb=pb, m=m: e.tensor_copy(out=yT[:, m, :], in_=pb[:, 0:TH]), [pb], [yT])
            emit_rstd(S, C, PP, yT, rstd, ones, sq, TH)
            resid(ggt1, False)
            emit_rstd(S, C, PP, xs, rstd, ones, sq, TH)
            for k in range(KC):
                q = sq[k % 2]
                C.stt(q[:, :], xs[:, k, :], gs2[:, k:k + 1], rstd[:, :], ALU.mult, ALU.mult, [xs, gs2, rstd], [q])
                C.act(hT[:, k, :], q[:, :], AF.Identity, [q, modB], [hT], bias=modB[:, 16 + k:17 + k])
            for f in range(64):
                w = wslab(w1_d.ap()[f])
                pb = PP.get()
                for k in range(KC):
                    C.mm(pb[:, 0:TH], w[:, k, :], hT[:, k, :], [w, hT], [pb], start=(k == 0), stop=(k == KC - 1))
                q = sq[f % 2]
                C.act(q[:, :], pb[:, 0:TH], AF.Relu, [pb], [q])
                C.tt("dve", uT[:, f, :], q[:, :], q[:, :], ALU.mult, [q], [uT])
            for m in range(16):
                pb = PP.get()
                for half in range(2):
                    w = w2slab(w2_d.ap()[m, :, half * 32:(half + 1) * 32, :])
                    for k in range(32):
                        kk = half * 32 + k
                        C.mm(pb[:, 0:TH], w[:, k, :], uT[:, kk, :], [w, uT], [pb], start=(kk == 0), stop=(kk == 63))
                S.ev(lambda e, pb=pb, m=m: e.copy(out=yT[:, m, :], in_=pb[:, 0:TH]), lambda e, pb=pb, m=m: e.tensor_copy(out=yT[:, m, :], in_=pb[:, 0:TH]), [pb], [yT])
            emit_rstd(S, C, PP, yT, rstd, ones, sq, TH)
            resid(ggt2, True)
            for k in range(KC):
                C.dma(xo_d.ap()[k * 128:(k + 1) * 128, hs], xs[:, k, :], reads=[xs], isout=True)
        S.finish()
        S.emit()
    return nc


def _slab(W, nk, nm):
    return np.ascontiguousarray(W.reshape(nk, 128, nm, 128).transpose(2, 1, 0, 3))


def _cols(v, n):
    return np.asarray(v, np.float32).reshape(n, 128).T


def _run(nc, in_maps):
    res = run_bass_kernel_spmd(nc, in_maps, core_ids=list(range(len(in_maps))))
    return res.results


def forward(inp, T=1024, ncores=NCORES, TB=512, runner=_run):
    f = lambda k: np.asarray(inp[k], np.float32)
    x = f("x")[0]
    TS = x.shape[0]
    cT = np.ascontiguousarray(f("c")[0].reshape(KC, 128).T)
    xT = [np.ascontiguousarray(x[j * T:(j + 1) * T].T) for j in range(ncores)]
    P = {k: f(k) for k in ("gla_w_a_up", "gla_b_a", "gla_norm_w", "rwkv_mu", "rwkv_w0", "rwkv_w_up", "rwkv_a0", "rwkv_a_up", "rwkv_g_up",
                           "rwkv_k_k", "rwkv_k_a", "rwkv_r_k", "rwkv_gn_w", "rwkv_gn_b", "vres_mu", "vres_up", "vres_v0")}
    vfirstT = None
    for l in range(2):
        nz = 53 if l == 0 else 54
        wada = _slab(f("w_ada")[l], KC, 96)
        b = _cols(f("b_ada")[l], 96)
        vecsA = np.ascontiguousarray(np.concatenate([b[:, :32], _cols(f("g_pre_mix")[l], KC)], 1))
        wfull = f("w_in")[l] if l == 0 else np.concatenate([f("w_in")[l], f("vres_w_down")[0]], 1)
        m = p1_colmap()[: nz * 128]
        Wp = np.zeros((D, nz * 128), np.float32)
        Wp[:, m >= 0] = wfull[:, m[m >= 0]]
        winp = _slab(Wp, KC, nz)
        wadaA = np.ascontiguousarray(wada[:32])
        nc1 = build_p1(T, nz)
        r1 = runner(nc1, [{"xT": xT[j], "cT": cT, "wada": wadaA, "vecs": vecsA, "winp": winp} for j in range(ncores)])
        zfull = np.zeros((NZ * 128, TS), np.float32)
        for j in range(ncores):
            zfull[: nz * 128, j * T:(j + 1) * T] = r1[j]["zT"]
        del winp, Wp, r1
        nc2 = build_p2(TS, TB, l)
        r2 = runner(nc2, [p2_inputs(zfull, P, l, j, TB, vfirstT) for j in range(8)])
        oT = np.zeros((D, TS), np.float32)
        for j in range(8):
            oT[j * 128:(j + 1) * 128] = r2[j]["og"]
            oT[1024 + j * 128:1024 + (j + 1) * 128] = r2[j]["orw"]
        if l == 0:
            vfirstT = np.concatenate([r2[j]["vfout"] for j in range(8)], 0)
        del zfull, r2
        vecsB = np.ascontiguousarray(np.concatenate([b[:, 32:], _cols(f("g_post_mix")[l], KC), _cols(f("g_pre_ffn")[l], KC), _cols(f("g_post_ffn")[l], KC)], 1))
        wadaB = np.ascontiguousarray(wada[32:])
        wout = _slab(f("w_out")[l], KC, 16)
        w1 = _slab(f("w_ff1")[l], KC, 64)
        w2 = _slab(f("w_ff2")[l], 64, 16)
        nc3 = build_p3(T)
        r3 = runner(nc3, [{"xT": xT[j], "oT": np.ascontiguousarray(oT[:, j * T:(j + 1) * T]), "cT": cT, "wada": wadaB, "vecs": vecsB,
                           "wout": wout, "w1": w1, "w2": w2} for j in range(ncores)])
        xT = [r3[j]["xo"] for j in range(ncores)]
        del wout, w1, w2, wada, r3
    out = np.concatenate([xT[j].T for j in range(ncores)], 0)[None]
    return np.ascontiguousarray(out.astype(np.float32))


def kernel(**inputs):
    return forward(inputs)
```
